# Optimizing a Trainium2 kernel written in Bass

```python
import jax, jax.numpy as jnp
from jax import lax
import numpy as np

D_MODEL = 1024
BATCH = 8
SEQ = 2048
DEPTH = 4

N_A_LAYERS = DEPTH // 2
N_B_LAYERS = DEPTH - N_A_LAYERS
GDN_HEADS = D_MODEL // 128
GDN_DK = 128
GDN_DV = 128
GDN_CONV = 4
GDN_CHUNK = 64
SWA_HEAD_DIM = 64
SWA_Q_HEADS = D_MODEL // SWA_HEAD_DIM
SWA_KV_HEADS = 4
SWA_WINDOW = 128
SWA_BLOCK = 128
D_FF = 4 * D_MODEL
NORM_EPS = 1e-6
MOD_STD = 0.1

GDN_KW = GDN_HEADS * GDN_DK
GDN_VW = GDN_HEADS * GDN_DV
GDN_IN = 2 * GDN_KW + 2 * GDN_VW + 2 * GDN_HEADS
SWA_QW = SWA_Q_HEADS * SWA_HEAD_DIM
SWA_KVW = SWA_KV_HEADS * SWA_HEAD_DIM

kernel_name = "yoco_gdn_swa_sink_alibi_sandwich_adaln"


def rms_norm(x, gain):
    xf = x.astype(jnp.float32)
    y = xf * lax.rsqrt(jnp.mean(xf * xf, axis=-1, keepdims=True) + NORM_EPS)
    return (y * gain.astype(jnp.float32)).astype(x.dtype)


def l2norm(x):
    xf = x.astype(jnp.float32)
    return (xf * lax.rsqrt(jnp.sum(xf * xf, axis=-1, keepdims=True) + NORM_EPS)).astype(x.dtype)


def modulation(c, w, b):
    return jax.nn.silu(c) @ w + b


def adaln_pre(x, gain, shift, scale):
    return rms_norm(x, gain) * (1.0 + scale[:, None, :]) + shift[:, None, :]


def adaln_post(x, y, gain, gate):
    return x + (1.0 + gate[:, None, :]) * rms_norm(y, gain)


def causal_depthwise_conv(x, w):
    K, C = w.shape
    return lax.conv_general_dilated(x, w[:, None, :].astype(x.dtype), window_strides=(1,),
                                    padding=((K - 1, 0),),
                                    dimension_numbers=('NWC', 'WIO', 'NWC'),
                                    feature_group_count=C)


def gated_delta_rule(q, k, v, g, beta):
    out_dtype = v.dtype
    Bn, Sn, H, dk = q.shape
    dv = v.shape[-1]
    C = GDN_CHUNK
    N = Sn // C
    f32 = jnp.float32

    def chunk(t):
        t = t.astype(f32).reshape((Bn, N, C, H) + t.shape[3:])
        return jnp.moveaxis(t, 3, 1)

    q = chunk(q) * (dk ** -0.5)
    k = chunk(k)
    v = chunk(v)
    beta = chunk(beta)
    g = jnp.cumsum(chunk(g), axis=-1)
    idx = jnp.arange(C)
    incl = idx[:, None] >= idx[None, :]
    strict = idx[:, None] > idx[None, :]
    decay = jnp.exp(jnp.where(incl, g[..., :, None] - g[..., None, :], -jnp.inf))
    k_beta = k * beta[..., None]
    v_beta = v * beta[..., None]
    L = jnp.where(strict, jnp.einsum('bhncd,bhnsd->bhncs', k_beta, k) * decay, 0.0)
    eye = jnp.eye(C, dtype=f32)
    T = lax.linalg.triangular_solve(eye + L, jnp.broadcast_to(eye, L.shape), left_side=True,
                                    lower=True, unit_diagonal=True)
    u = jnp.einsum('bhncs,bhnse->bhnce', T, v_beta)
    w = jnp.einsum('bhncs,bhnsd->bhncd', T, k_beta * jnp.exp(g)[..., None])
    attn_intra = jnp.einsum('bhncd,bhnsd->bhncs', q, k) * decay
    q_g = q * jnp.exp(g)[..., None]
    k_g = k * jnp.exp(g[..., -1:] - g)[..., None]
    g_last = jnp.exp(g[..., -1])

    def step(S, inp):
        u_n, w_n, qg_n, kg_n, a_n, gl_n = inp
        v_new = u_n - jnp.einsum('bhcd,bhde->bhce', w_n, S)
        o = jnp.einsum('bhcd,bhde->bhce', qg_n, S) + jnp.einsum('bhcs,bhse->bhce', a_n, v_new)
        S = S * gl_n[..., None, None] + jnp.einsum('bhcd,bhce->bhde', kg_n, v_new)
        return S, o

    xs = tuple(jnp.moveaxis(t, 2, 0) for t in (u, w, q_g, k_g, attn_intra, g_last))
    S0 = jnp.zeros((Bn, H, dk, dv), f32)
    _, o = lax.scan(step, S0, xs)
    o = jnp.moveaxis(jnp.moveaxis(o, 0, 2), 1, 3)
    return o.reshape(Bn, Sn, H, dv).astype(out_dtype)


def gdn_mixer(h, w_in, conv_w, a_log, dt_bias, onorm, w_out):
    Bn, Sn, _ = h.shape
    proj = h @ w_in
    qkv, z, a, b = jnp.split(proj, [2 * GDN_KW + GDN_VW, 2 * GDN_KW + 2 * GDN_VW,
                                    2 * GDN_KW + 2 * GDN_VW + GDN_HEADS], axis=-1)
    qkv = jax.nn.silu(causal_depthwise_conv(qkv, conv_w))
    q, k, v = jnp.split(qkv, [GDN_KW, 2 * GDN_KW], axis=-1)
    q = l2norm(q.reshape(Bn, Sn, GDN_HEADS, GDN_DK))
    k = l2norm(k.reshape(Bn, Sn, GDN_HEADS, GDN_DK))
    v = v.reshape(Bn, Sn, GDN_HEADS, GDN_DV)
    g = -jnp.exp(a_log.astype(jnp.float32)) * jax.nn.softplus(a.astype(jnp.float32) + dt_bias.astype(jnp.float32))
    beta = jax.nn.sigmoid(b.astype(jnp.float32))
    o = gated_delta_rule(q, k, v, g, beta)
    o = rms_norm(o, onorm) * jax.nn.silu(z.reshape(Bn, Sn, GDN_HEADS, GDN_DV))
    return o.reshape(Bn, Sn, GDN_VW) @ w_out


def alibi_slopes(n):
    return jnp.exp2(-8.0 * jnp.arange(1, n + 1, dtype=jnp.float32) / n)


def swa_sink_attention(q, k, v, sinks):
    Bn, Sn, Hq, hd = q.shape
    Hkv = k.shape[2]
    G = Hq // Hkv
    W = SWA_BLOCK
    N = Sn // W
    qb = q.reshape(Bn, N, W, Hkv, G, hd)

    def band(t):
        tb = t.reshape(Bn, N, W, Hkv, hd)
        prev = jnp.pad(tb, ((0, 0), (1, 0), (0, 0), (0, 0), (0, 0)))[:, :-1]
        return jnp.concatenate([prev, tb], axis=2)

    kb = band(k)
    vb = band(v)
    scores = jnp.einsum('bnqhgd,bnkhd->bnhgqk', qb, kb).astype(jnp.float32) * (hd ** -0.5)
    dist = (jnp.arange(W)[:, None] + W) - jnp.arange(2 * W)[None, :]
    valid = (dist >= 0) & (dist < SWA_WINDOW)
    blk = jnp.arange(N)
    valid = valid[None] & ((blk[:, None, None] > 0) | (jnp.arange(2 * W) >= W)[None, None, :])
    slopes = alibi_slopes(Hq).reshape(Hkv, G)
    scores = scores - slopes[:, :, None, None] * dist.astype(jnp.float32)
    scores = jnp.where(valid[None, :, None, None], scores, -jnp.inf)
    sink = sinks.astype(jnp.float32).reshape(Hkv, G)[:, :, None, None]
    m = jnp.maximum(jnp.max(scores, axis=-1, keepdims=True), sink)
    p = jnp.exp(scores - m)
    p = p / (jnp.sum(p, axis=-1, keepdims=True) + jnp.exp(sink - m))
    o = jnp.einsum('bnhgqk,bnkhd->bnqhgd', p.astype(v.dtype), vb)
    return o.reshape(Bn, Sn, Hq, hd)


def swa_mixer(h, w_q, k_sh, v_sh, sinks, w_o):
    Bn, Sn, _ = h.shape
    q = (h @ w_q).reshape(Bn, Sn, SWA_Q_HEADS, SWA_HEAD_DIM)
    o = swa_sink_attention(q, k_sh, v_sh, sinks)
    return o.reshape(Bn, Sn, SWA_QW) @ w_o


def sq_relu_mlp(h, w1, w2):
    return jnp.square(jax.nn.relu(h @ w1)) @ w2


def shared_kv(x, c, kv_mod_w, kv_mod_b, kv_norm, w_kv):
    Bn, Sn, _ = x.shape
    shift, scale = jnp.split(modulation(c, kv_mod_w, kv_mod_b), 2, axis=-1)
    h = adaln_pre(x, kv_norm, shift, scale)
    k, v = jnp.split(h @ w_kv, 2, axis=-1)
    return (k.reshape(Bn, Sn, SWA_KV_HEADS, SWA_HEAD_DIM),
            v.reshape(Bn, Sn, SWA_KV_HEADS, SWA_HEAD_DIM))


def setup_inputs(seed: int = 0) -> dict:
    key = jax.random.key(seed)
    ks = jax.random.split(key, 24)
    f32 = jnp.float32
    D = D_MODEL

    def nrm(k, shape, std):
        return jax.random.normal(k, shape, f32) * std

    dt = jnp.exp(jax.random.uniform(ks[6], (N_A_LAYERS, GDN_HEADS), f32, np.log(1e-3), np.log(1e-1)))
    return {
        "x": nrm(ks[0], (BATCH, SEQ, D), 1.0),
        "c": nrm(ks[1], (BATCH, D), 1.0),
        "mod_w": nrm(ks[2], (DEPTH, D, 6 * D), MOD_STD * D ** -0.5),
        "mod_b": nrm(ks[3], (DEPTH, 6 * D), 0.02),
        "norm_g": 1.0 + nrm(ks[4], (DEPTH, 4, D), 0.05),
        "gdn_w_in": nrm(ks[5], (N_A_LAYERS, D, GDN_IN), D ** -0.5),
        "gdn_conv": nrm(ks[7], (N_A_LAYERS, GDN_CONV, 2 * GDN_KW + GDN_VW), GDN_CONV ** -0.5),
        "gdn_a_log": jnp.log(jax.random.uniform(ks[8], (N_A_LAYERS, GDN_HEADS), f32, 1.0, 16.0)),
        "gdn_dt_bias": dt + jnp.log(-jnp.expm1(-dt)),
        "gdn_onorm": 1.0 + nrm(ks[9], (N_A_LAYERS, GDN_DV), 0.05),
        "gdn_w_out": nrm(ks[10], (N_A_LAYERS, GDN_VW, D), GDN_VW ** -0.5),
        "kv_mod_w": nrm(ks[11], (D, 2 * D), MOD_STD * D ** -0.5),
        "kv_mod_b": nrm(ks[12], (2 * D,), 0.02),
        "kv_norm": 1.0 + nrm(ks[13], (D,), 0.05),
        "w_kv": nrm(ks[14], (D, 2 * SWA_KVW), D ** -0.5),
        "swa_w_q": nrm(ks[15], (N_B_LAYERS, D, SWA_QW), D ** -0.5),
        "swa_sinks": nrm(ks[16], (N_B_LAYERS, SWA_Q_HEADS), 0.5),
        "swa_w_o": nrm(ks[17], (N_B_LAYERS, SWA_QW, D), SWA_QW ** -0.5),
        "mlp_w1": nrm(ks[18], (DEPTH, D, D_FF), D ** -0.5),
        "mlp_w2": nrm(ks[19], (DEPTH, D_FF, D), D_FF ** -0.5),
    }


def reference(x, c, mod_w, mod_b, norm_g, gdn_w_in, gdn_conv, gdn_a_log, gdn_dt_bias, gdn_onorm,
              gdn_w_out, kv_mod_w, kv_mod_b, kv_norm, w_kv, swa_w_q, swa_sinks, swa_w_o,
              mlp_w1, mlp_w2):
    k_sh = None
    v_sh = None
    for layer in range(DEPTH):
        s_mix, sc_mix, g_mix, s_mlp, sc_mlp, g_mlp = jnp.split(
            modulation(c, mod_w[layer], mod_b[layer]), 6, axis=-1)
        h = adaln_pre(x, norm_g[layer, 0], s_mix, sc_mix)
        if layer < N_A_LAYERS:
            i = layer
            y = gdn_mixer(h, gdn_w_in[i], gdn_conv[i], gdn_a_log[i], gdn_dt_bias[i],
                          gdn_onorm[i], gdn_w_out[i])
        else:
            j = layer - N_A_LAYERS
            y = swa_mixer(h, swa_w_q[j], k_sh, v_sh, swa_sinks[j], swa_w_o[j])
        x = adaln_post(x, y, norm_g[layer, 1], g_mix)
        h = adaln_pre(x, norm_g[layer, 2], s_mlp, sc_mlp)
        x = adaln_post(x, sq_relu_mlp(h, mlp_w1[layer], mlp_w2[layer]), norm_g[layer, 3], g_mlp)
        if layer == N_A_LAYERS - 1:
            k_sh, v_sh = shared_kv(x, c, kv_mod_w, kv_mod_b, kv_norm, w_kv)
    return x
```

```python
import numpy as np
from contextlib import ExitStack
import concourse.bass as bass
import concourse.mybir as mybir
from concourse.bass_utils import run_bass_kernel_spmd

F32 = mybir.dt.float32
BF16 = mybir.dt.bfloat16
AF = mybir.ActivationFunctionType
ALU = mybir.AluOpType
AX = mybir.AxisListType

ENGS = ["pe", "act", "dve", "pool", "sp"]
NDMA = 24

D = 1024
SEQ = 2048
NT = 1024
NTT = NT // 512
NCK = NT // 128
EPS = 1e-6
NEG = -30000.0


class Sched:
    def __init__(self, nc, es):
        self.nc = nc
        self.ops = {e: [] for e in ENGS}
        self.count = {e: 0 for e in ENGS}
        self.known = {e: {} for e in ENGS}
        self.track = {}
        self.pending = {e: [] for e in ENGS}
        self.sems = {}
        for e in ["pe", "act", "dve", "pool"]:
            self.sems[e] = es.enter_context(nc.semaphore("s_" + e))
        self.dma_n = {"sp": 0, "pool": 0}
        self.dma_uses = {}
        for q in ["sp", "pool"]:
            for i in range(NDMA):
                self.sems[("dma", q, i)] = es.enter_context(nc.semaphore(f"d_{q}_{i}"))
                self.dma_uses[(q, i)] = 0
        self.last_dma_events = {}
        self.nops = 0
        self.keep = set()
        self.paranoid = False

    def _conflicts(self, key):
        d = self.track.setdefault(key[0], {})
        out = []
        for k2, st in d.items():
            n = min(len(k2), len(key))
            if k2[:n] == key[:n]:
                out.append(st)
        return d, out

    def _deps(self, reads, writes, ev):
        deps = []
        for k in reads:
            k = k if isinstance(k, tuple) else (k,)
            d, cs = self._conflicts(k)
            for st in cs:
                if st[0] is not None:
                    deps.append(st[0])
            st = d.setdefault(k, [None, []])
            st[1].append(ev)
        for k in writes:
            k = k if isinstance(k, tuple) else (k,)
            d, cs = self._conflicts(k)
            for st in cs:
                if st[0] is not None:
                    deps.append(st[0])
                deps.extend(st[1])
            for k2 in [k2 for k2 in d if len(k2) >= len(k) and k2[:len(k)] == k]:
                del d[k2]
            d[k] = [ev, []]
        return deps

    def _waits(self, eng, deps, ev):
        need = {}
        for (sk, val, deng) in deps:
            if (sk, val) == (ev[0], ev[1]):
                continue
            if deng == eng and eng == "pe":
                continue
            if val > need.get(sk, 0):
                need[sk] = val
        out = []
        kn = self.known[eng]
        for sk, val in need.items():
            if kn.get(sk, 0) >= val:
                continue
            kn[sk] = val
            out.append((sk, val))
        return out

    def op(self, eng, fn, reads=(), writes=()):
        self.count[eng] += 1
        ev = (eng, self.count[eng], eng)
        deps = self._deps(reads, writes, ev) + self.pending[eng]
        if self.paranoid:
            if self.paranoid & 1:
                deps = deps + [(e2, self.count[e2], e2) for e2 in ["pe", "act", "dve"] if self.count[e2] > 0 and e2 != eng]
            if (self.paranoid & 2) and self.count[eng] > 1 and eng != "pe":
                deps.append((eng, self.count[eng] - 1, eng + "_self"))
            if (self.paranoid & 4) and self.count[eng] > 1 and eng == "pe":
                deps.append((eng, self.count[eng] - 1, eng + "_self"))
        self.pending[eng] = []
        waits = self._waits(eng, deps, ev)
        self.ops[eng].append((fn, waits, (eng, 1)))
        self.nops += 1
        return ev

    def dma(self, q, fn, reads=(), writes=()):
        i = self.dma_n[q] % NDMA
        self.dma_n[q] += 1
        self.dma_uses[(q, i)] += 1
        k = self.dma_uses[(q, i)]
        sk = ("dma", q, i)
        ev = (sk, 16 * k, "dma_" + q)
        deps = self._deps(reads, writes, ev) + self.pending[q]
        self.pending[q] = []
        if k > 1:
            deps.append((sk, 16 * (k - 1), "dma_" + q))
        waits = self._waits(q, deps, ev)
        self.ops[q].append((fn, waits, (sk, 16)))
        self.last_dma_events[sk] = ev
        self.nops += 1
        return ev

    def barrier(self, keep=()):
        evs = [(e, self.count[e], e) for e in ["pe", "act", "dve", "pool"] if self.count[e] > 0]
        evs += list(self.last_dma_events.values())
        for e in ENGS:
            if e == "pool":
                continue
            self.pending[e] = self.pending[e] + evs
        self.track = {k: v for k, v in self.track.items() if k in self.keep}

    def finish(self):
        self.barrier()
        for e in ["pool"]:
            self.pending[e] = []
        deps = self.pending["sp"]
        self.pending["sp"] = []
        waits = self._waits("sp", deps, ("none", 0, "none"))
        self.ops["sp"].append((None, waits, None))

    def emit(self):
        nc = self.nc
        sems = self.sems

        def run(engname, eng):
            for fn, waits, inc in self.ops[engname]:
                for sk, val in waits:
                    eng.wait_ge(sems[sk], val)
                if fn is None:
                    continue
                ins = fn(eng)
                if inc is not None:
                    ins.then_inc(sems[inc[0]], inc[1])

        with nc.Block() as block:
            @block.tensor
            def _(e):
                run("pe", e)

            @block.scalar
            def _(e):
                run("act", e)

            @block.vector
            def _(e):
                run("dve", e)

            @block.gpsimd
            def _(e):
                run("pool", e)

            @block.sync
            def _(e):
                run("sp", e)


class Tl:
    def __init__(self, name, ap):
        self.name = name
        self.ap = ap

    def __getitem__(self, k):
        return self.ap[k]


class Arena:
    def __init__(self, handle, words):
        self.h = handle
        self.words = words
        self.top = 0
        self.n = 0

    def alloc(self, name, shape, dt):
        n = 1
        for s in shape[1:]:
            n *= s
        w = n if dt == F32 else (n + 1) // 2
        w = (w + 7) // 8 * 8
        off = self.top
        self.top += w
        assert self.top <= self.words, f"arena overflow at {name}: {self.top} > {self.words}"
        ap = self.h[:, off:off + w]
        if dt != F32:
            ap = ap.bitcast(dt)
        ap = ap[:, 0:n]
        if len(shape) == 3:
            ap = ap.rearrange("p (a b) -> p a b", a=shape[1])
        elif len(shape) == 4:
            ap = ap.rearrange("p (a b c) -> p a b c", a=shape[1], b=shape[2])
        if shape[0] < 128:
            ap = ap[0:shape[0]]
        self.n += 1
        return Tl(f"{name}#{self.n}", ap)

    def mark(self):
        return self.top

    def release(self, m):
        self.top = m


def _cst_layout():
    off = {}
    o = 0
    for name, n in [("cT", 8), ("modb", 192), ("kvmodb", 16), ("ng", 128), ("kvng", 8), ("convw", 192),
                    ("onorm", 2), ("alog", 16), ("dtb", 16), ("sinks", 32), ("ident", 128), ("triU", 128),
                    ("mnS", 128), ("mnUI", 128)]:
        off[name] = (o, n)
        o += n
    return off, o


CST_OFF, NCST = _cst_layout()

DBG = {"swa": 99, "paranoid": 1}
ALL_STAGES = ["mix0", "mlp0", "mix1", "mlp1", "kv", "mix2", "mlp2", "mix3", "mlp3"]


def build_program(stages=None, halves=(0, 1)):
    stages = list(ALL_STAGES if stages is None else stages)
    nc = bass.Bass("TRN2", target_bir_lowering=False)

    def din(name, shape):
        return nc.dram_tensor(name, list(shape), F32, kind="ExternalInput").ap()

    d_xT = din("xT", [8, 128, SEQ])
    d_cst = din("cst", [128, NCST])
    d_alibi = din("alibi", [128, 16 * 256])
    d_modw = din("mod_w", [4, D, 6 * D])
    d_kvmodw = din("kv_mod_w", [D, 2 * D])
    d_win = din("w_in_h", [2, 8, D, 512])
    d_wab = din("w_ab", [2, D, 16])
    d_wout = din("w_out", [2, D, D])
    d_wkd = din("w_kd", [D, 512])
    d_wv = din("w_v", [D, 256])
    d_wq = din("w_q", [2, D, D])
    d_wo = din("w_o", [2, D, D])
    d_w1 = din("w1", [4, D, 4 * D])
    d_w2r = din("w2r", [4, 8, 128, 32, 128])
    d_out = nc.dram_tensor("outT", [8, 128, SEQ], F32, kind="ExternalOutput").ap()

    es = ExitStack()
    with es:
        S = Sched(nc, es)
        AW = 53000
        arena_h = es.enter_context(nc.sbuf_tensor("arena", [128, AW], F32))
        psum_h = es.enter_context(nc.psum_tensor("psum", [128, 4096], F32))
        A = Arena(arena_h, AW)

        bank_ctr = [0]

        def bank():
            b = bank_ctr[0] % 8
            bank_ctr[0] += 1
            return b

        def bank2():
            if bank_ctr[0] % 2:
                bank_ctr[0] += 1
            b = bank_ctr[0] % 8
            bank_ctr[0] += 2
            return b

        def PS(b, n=512, off=0):
            return psum_h[:, b * 512 + off: b * 512 + off + n]

        def PSB(b):
            return psum_h[:, b * 512:(b + 1) * 512].bitcast(BF16)

        def pk(b):
            return ("ps", b)

        xT = A.alloc("xT", [128, 8, NT], F32)
        cst = A.alloc("cst", [128, NCST], F32)
        modv = A.alloc("modv", [128, 4 * 48 + 16], F32)
        coef = A.alloc("coef", [128, 5 * 4 * 8], F32)
        onesD = A.alloc("onesD", [128, 128], BF16)
        ones1 = A.alloc("ones1", [128, 128], BF16)
        onesV = A.alloc("onesV", [128, 128], BF16)
        identb = A.alloc("identb", [128, 128], BF16)
        csb = A.alloc("csb", [128, 8], BF16)
        NWA = 3
        wA = [A.alloc(f"wA{i}", [128, 8, 512], BF16) for i in range(NWA)]
        NWB = 4
        wB = [A.alloc(f"wB{i}", [128, 16, 128], BF16) for i in range(NWB)]
        wab = A.alloc("wab", [128, 8, 16], BF16)
        S.keep = {t.name for t in wA + wB + [wab]}
        Scar = A.alloc("Scar", [128, 16, 128], F32)
        halo = A.alloc("halo", [128, 48, 3], F32)
        kcar = A.alloc("kcar", [128, 4, 128], BF16)
        vcar = A.alloc("vcar", [128, 256], BF16)
        epsb = A.alloc("epsb", [128, 1], F32)
        kv_mark = A.mark()
        kTd = A.alloc("kTd", [128, 4, 128 + NT], BF16)
        vtok = A.alloc("vtok", [128, NCK + 1, 256], BF16)
        persist_mark = A.mark()

        def C(name, a=0, b=None):
            o, n = CST_OFF[name]
            b = n if b is None else b
            return cst[:, o + a:o + b]

        ident = C("ident")
        wa_ctr = [0]
        wb_ctr = [0]

        def next_wA():
            t = wA[wa_ctr[0] % NWA]
            wa_ctr[0] += 1
            return t

        def next_wB():
            t = wB[wb_ctr[0] % NWB]
            wb_ctr[0] += 1
            return t

        def mmgroup(out, pairs):
            n = len(pairs)

            def f(e):
                r = None
                for i, (l, r_) in enumerate(pairs):
                    r = e.matmul(out, lhsT=l, rhs=r_, start=(i == 0), stop=(i == n - 1))
                return r
            return f

        def act(out, in_, func, reads, writes, scale=None, bias=None, accum=None):
            kw = {}
            if scale is not None:
                kw["scale"] = scale
            if bias is not None:
                kw["bias"] = bias
            if accum is not None:
                kw["accum_out"] = accum
            S.op("act", lambda e: e.activation(out=out, in_=in_, func=func, **kw), reads=reads, writes=writes)

        def tt(out, in0, in1, op, reads, writes, eng="dve"):
            S.op(eng, lambda e: e.tensor_tensor(out=out, in0=in0, in1=in1, op=op), reads=reads, writes=writes)

        def stt(out, in0, scalar, in1, op0, op1, reads, writes):
            S.op("dve", lambda e: e.scalar_tensor_tensor(out=out, in0=in0, scalar=scalar, in1=in1, op0=op0, op1=op1),
                 reads=reads, writes=writes)

        def ts(out, in0, s1, op0, reads, writes, s2=None, op1=None, eng="dve"):
            if op1 is None:
                S.op(eng, lambda e: e.tensor_scalar(out=out, in0=in0, scalar1=s1, scalar2=None, op0=op0),
                     reads=reads, writes=writes)
            else:
                S.op(eng, lambda e: e.tensor_scalar(out=out, in0=in0, scalar1=s1, scalar2=s2, op0=op0, op1=op1),
                     reads=reads, writes=writes)

        def dma_w(slot_ap, src, key):
            S.dma("pool", lambda e: e.dma_start(out=slot_ap, in_=src), writes=[key])

        bg_jobs = []

        def bg(n=1):
            for _ in range(n):
                if bg_jobs:
                    bg_jobs.pop(0)()

        def bg_flush(tag):
            while bg_jobs and any(getattr(j, "tag", None) == tag for j in bg_jobs):
                bg_jobs.pop(0)()

        def mod_jobs(li):
            nblk = 12 if li < 4 else 4
            src_all = (d_modw[li] if li < 4 else d_kvmodw).rearrange("(kc p) n -> p kc n", p=128)
            col0 = li * 48
            bias_ap = (lambda a, b: C("modb", li * 48 + a, li * 48 + b)) if li < 4 else (lambda a, b: C("kvmodb", a, b))
            jobs = []
            for blk in range(nblk):
                def job(blk=blk):
                    slot = next_wA()
                    dma_w(slot[:, :, :], src_all[:, :, blk * 512:(blk + 1) * 512], slot.name)
                    b = bank()
                    for jj in range(4):
                        S.op("pe", mmgroup(PS(b, 1, jj), [(slot[:, kc, jj * 128:(jj + 1) * 128], csb[:, kc:kc + 1]) for kc in range(8)]),
                             reads=[slot.name, csb.name], writes=[pk(b)])
                    tt(modv[:, col0 + blk * 4: col0 + blk * 4 + 4], PS(b, 4), bias_ap(blk * 4, blk * 4 + 4), ALU.add,
                       reads=[pk(b), cst.name], writes=[(modv.name, li, blk)])
                    if blk == nblk - 1:
                        cf = lambda k: coef[:, (li * 4 + k) * 8:(li * 4 + k + 1) * 8]
                        mv = lambda j: modv[:, col0 + j * 8: col0 + (j + 1) * 8]
                        if li < 4:
                            ngl = lambda n_: C("ng", (li * 4 + n_) * 8, (li * 4 + n_ + 1) * 8)
                            for k, (j, n_) in enumerate([(1, 0), (2, 1), (4, 2), (5, 3)]):
                                stt(cf(k), mv(j), 1.0, ngl(n_), ALU.add, ALU.mult,
                                    reads=[(modv.name, li), cst.name], writes=[(coef.name, li, k)])
                        else:
                            stt(cf(0), mv(1), 1.0, C("kvng"), ALU.add, ALU.mult,
                                reads=[(modv.name, li), cst.name], writes=[(coef.name, li, 0)])
                job.tag = li
                jobs.append(job)
            return jobs

        def coefA(li, which):
            return coef[:, (li * 4 + which) * 8:(li * 4 + which + 1) * 8]

        def shiftB(li, sub):
            j = 0 if sub == 0 else 3
            return modv[:, li * 48 + j * 8: li * 48 + (j + 1) * 8]

        def sumsq_rstd(src_tile, src_key, cols, ones_t, sq, rt, rs, nchunks=8, src3=True, b=None):
            b = bank() if b is None else b
            for kc in range(nchunks):
                sqt = sq[kc % 2]
                src = src_tile[:, kc, cols] if src3 else src_tile[:, cols]
                act(sqt[:, :], src, AF.Square, reads=[src_key(kc)], writes=[sqt.name])
                S.op("pe", lambda e, sqt=sqt, kc=kc: e.matmul(PS(b), lhsT=ones_t[:, :], rhs=sqt[:, :], start=(kc == 0),
                                                              stop=(kc == nchunks - 1)),
                     reads=[sqt.name, ones_t.name], writes=[pk(b)])
            act(rt[:, :], PS(b), AF.Sqrt, reads=[pk(b)], writes=[rt.name], bias=epsb[:, 0:1])
            S.op("dve", lambda e: e.reciprocal(out=rs[:, :], in_=rt[:, :]), reads=[rt.name], writes=[rs.name])

        def prenorm(hT, li, which, sub, tmp):
            Aap = coefA(li, which)
            Bap = shiftB(li, sub) if li < 4 else modv[:, 192:200]
            for t in range(NTT):
                cols = slice(t * 512, (t + 1) * 512)
                sumsq_rstd(xT, lambda kc: (xT.name, kc, t), cols, onesD, tmp["sq"], tmp["rt"], tmp["rs"])
                for kc in range(8):
                    tm = tmp["f"][kc % 2]
                    stt(tm[:, :], xT[:, kc, cols], Aap[:, kc:kc + 1], tmp["rs"][:, :], ALU.mult, ALU.mult,
                        reads=[(xT.name, kc, t), tmp["rs"].name, coef.name], writes=[tm.name])
                    act(hT[:, kc, cols], tm[:, :], AF.Identity, reads=[tm.name, modv.name], writes=[(hT.name, kc, t)],
                        bias=Bap[:, kc:kc + 1])

        def postnorm(yb, ycols_of, li, which, tmp, tiles=range(NTT)):
            Gap = coefA(li, which)
            for t in tiles:
                cols = slice(t * 512, (t + 1) * 512)
                yc = ycols_of(t)
                sumsq_rstd(yb, lambda kc: (yb.name, kc, t), yc, onesD, tmp["sq"], tmp["rt"], tmp["rs"])
                for kc in range(8):
                    tm = tmp["f"][kc % 2]
                    stt(tm[:, :], yb[:, kc, yc], Gap[:, kc:kc + 1], tmp["rs"][:, :], ALU.mult, ALU.mult,
                        reads=[(yb.name, kc, t), tmp["rs"].name, coef.name], writes=[tm.name])
                    tt(xT[:, kc, cols], xT[:, kc, cols], tm[:, :], ALU.add,
                       reads=[(xT.name, kc, t), tm.name], writes=[(xT.name, kc, t)])

        def alloc_tmp():
            return {"sq": [A.alloc("sq0", [128, 512], BF16), A.alloc("sq1", [128, 512], BF16)],
                    "rt": A.alloc("rt", [128, 512], F32), "rs": A.alloc("rs", [128, 512], F32),
                    "f": [A.alloc("f0", [128, 512], F32), A.alloc("f1", [128, 512], F32)]}

        def proj_fm(dst_fn, wsrc_rows, ncolblk, hT, nk=8):
            for cb in range(ncolblk):
                slot = next_wA()
                dma_w(slot[:, :, :], wsrc_rows[:, :, cb * 512:(cb + 1) * 512], slot.name)
                for fi in range(4):
                    f = cb * 4 + fi
                    for t in range(NTT):
                        b = bank()
                        S.op("pe", mmgroup(PS(b), [(slot[:, kc, fi * 128:(fi + 1) * 128], hT[:, kc, t * 512:(t + 1) * 512])
                                                   for kc in range(nk)]),
                             reads=[slot.name] + [(hT.name, kc, t) for kc in range(nk)], writes=[pk(b)])
                        dst_fn(f, t, PS(b), b)

        S.dma("sp", lambda e: e.dma_start(out=cst[:, :], in_=d_cst), writes=[cst.name])
        o_id = CST_OFF["ident"][0]
        S.dma("pool", lambda e: e.dma_start(out=identb[:, :], in_=d_cst[:, o_id:o_id + 128]), writes=[identb.name])
        S.op("dve", lambda e: e.memset(onesD[:, :], 1.0 / 1024.0), writes=[onesD.name])
        S.op("dve", lambda e: e.memset(ones1[:, :], 1.0), writes=[ones1.name])
        S.op("dve", lambda e: e.memset(onesV[:, :], 1.0 / 128.0), writes=[onesV.name])
        S.op("dve", lambda e: e.memset(epsb[:, :], EPS), writes=[epsb.name])
        S.op("dve", lambda e: e.memset(Scar[:, :, :], 0.0), writes=[Scar.name])
        S.op("dve", lambda e: e.memset(halo[:, :, :], 0.0), writes=[halo.name])
        act(csb[:, :], C("cT"), AF.Silu, reads=[cst.name], writes=[csb.name])

        need_layers = sorted({int(s[-1]) for s in stages if s != "kv"})
        need_kv = "kv" in stages
        first = True
        for li in need_layers:
            jobs = mod_jobs(li)
            if first:
                for j in jobs:
                    j()
                first = False
            else:
                bg_jobs.extend(jobs)
            if li == 1 and need_kv:
                bg_jobs.extend(mod_jobs(4))
        if need_kv and 1 not in need_layers:
            bg_jobs.extend(mod_jobs(4))

        def mlp_stage(li, half):
            m0 = A.mark()
            tmp = alloc_tmp()
            hT = A.alloc("hTm", [128, 8, NT], BF16)
            h1T = A.alloc("h1T", [128, 16, NT], BF16)
            yb = A.alloc("yb", [128, 8, NT], F32)
            rl = [A.alloc("rl0", [128, 512], F32), A.alloc("rl1", [128, 512], F32)]
            prenorm(hT, li, 2, 1, tmp)
            w1src = d_w1[li].rearrange("(kc p) n -> p kc n", p=128)
            rc = [0]
            for hh in range(2):
                def dst(f, t, ps, b, hh=hh):
                    r = rl[rc[0] % 2]
                    rc[0] += 1
                    act(r[:, :], ps, AF.Relu, reads=[pk(b)], writes=[r.name])
                    tt(h1T[:, f, t * 512:(t + 1) * 512], r[:, :], r[:, :], ALU.mult, reads=[r.name], writes=[(h1T.name, f, t)])
                for cb in range(4):
                    slot = next_wA()
                    blk = hh * 4 + cb
                    dma_w(slot[:, :, :], w1src[:, :, blk * 512:(blk + 1) * 512], slot.name)
                    for fi in range(4):
                        f = cb * 4 + fi
                        for t in range(NTT):
                            b = bank()
                            S.op("pe", mmgroup(PS(b), [(slot[:, kc, fi * 128:(fi + 1) * 128], hT[:, kc, t * 512:(t + 1) * 512])
                                                       for kc in range(8)]),
                                 reads=[slot.name] + [(hT.name, kc, t) for kc in range(8)], writes=[pk(b)])
                            dst(f, t, PS(b), b)
                    bg()
                for f in range(8):
                    slot = next_wB()
                    dma_w(slot[:, :, :], d_w2r[li, f][:, hh * 16:(hh + 1) * 16, :], slot.name)
                    for t in range(NTT):
                        b = bank()
                        S.op("pe", mmgroup(PS(b), [(slot[:, m, :], h1T[:, m, t * 512:(t + 1) * 512]) for m in range(16)]),
                             reads=[slot.name] + [(h1T.name, m, t) for m in range(16)], writes=[pk(b)])
                        if hh == 0:
                            act(yb[:, f, t * 512:(t + 1) * 512], PS(b), AF.Copy, reads=[pk(b)], writes=[(yb.name, f, t)])
                        else:
                            tt(yb[:, f, t * 512:(t + 1) * 512], PS(b), yb[:, f, t * 512:(t + 1) * 512], ALU.add,
                               reads=[pk(b), (yb.name, f, t)], writes=[(yb.name, f, t)])
                    bg()
            postnorm(yb, lambda t: slice(t * 512, (t + 1) * 512), li, 3, tmp)
            S.barrier()
            A.release(m0)

        def kv_stage(half):
            m0 = A.mark()
            tmp = alloc_tmp()
            hT = A.alloc("hTk", [128, 8, NT], BF16)
            if half == 1:
                S.op("dve", lambda e: e.tensor_copy(out=kTd[:, :, 0:128], in_=kcar[:, :, :]),
                     reads=[kcar.name], writes=[(kTd.name, "c")])
                S.op("dve", lambda e: e.tensor_copy(out=vtok[:, 0, :], in_=vcar[:, :]),
                     reads=[vcar.name], writes=[(vtok.name, 0)])
            prenorm(hT, 4, 0, 0, tmp)
            ksrc = d_wkd.rearrange("(kc p) n -> p kc n", p=128)

            def dstk(f, t, ps, b):
                act(kTd[:, f, 128 + t * 512: 128 + (t + 1) * 512], ps, AF.Copy, reads=[pk(b)], writes=[(kTd.name, "m", f, t)])
            proj_fm(dstk, ksrc, 1, hT)
            slot = next_wA()
            vsrc = d_wv.rearrange("(kc p) n -> p kc n", p=128)
            dma_w(slot[:, :, 0:256], vsrc, slot.name)
            for ck in range(NCK):
                b = bank()
                S.op("pe", mmgroup(PS(b, 256), [(hT[:, kc, ck * 128:(ck + 1) * 128], slot[:, kc, 0:256]) for kc in range(8)]),
                     reads=[slot.name] + [(hT.name, kc, ck // 4) for kc in range(8)], writes=[pk(b)])
                act(vtok[:, 1 + ck, :], PS(b, 256), AF.Copy, reads=[pk(b)], writes=[(vtok.name, 1 + ck)])
            if half == 0:
                S.op("dve", lambda e: e.tensor_copy(out=kcar[:, :, :], in_=kTd[:, :, NT:NT + 128]),
                     reads=[kTd.name], writes=[kcar.name])
                S.op("dve", lambda e: e.tensor_copy(out=vcar[:, :], in_=vtok[:, NCK, :]),
                     reads=[(vtok.name, NCK)], writes=[vcar.name])
            S.barrier()
            A.release(m0)

        def outproj_post(oT, wsrc, li, tmp, yb):
            def dsty(f, t, ps, b):
                act(yb[:, f, t * 512:(t + 1) * 512], ps, AF.Copy, reads=[pk(b)], writes=[(yb.name, f, t)])
            proj_fm(dsty, wsrc, 2, oT)
            postnorm(yb, lambda t: slice(t * 512, (t + 1) * 512), li, 1, tmp)

        def swa_stage(li, half):
            j = li - 2
            m0 = A.mark()
            tmp = alloc_tmp()
            qT = A.alloc("qTs", [128, 8, NT], BF16)
            oT = A.alloc("oTs", [128, 8, NT], BF16)
            m1 = A.mark()
            alibi = A.alloc("alibi", [128, 16, 256], F32)
            sc = [A.alloc(f"sc{i}", [128, 4, 256], F32) for i in range(2)]
            Pm = [A.alloc(f"P{i}", [128, 4, 256], BF16) for i in range(2)]
            Pn = [A.alloc(f"Pn{i}", [128, 4, 256], BF16) for i in range(2)]
            PT = [A.alloc(f"PT{i}", [128, 8, 128], BF16) for i in range(2)]
            sm = [A.alloc(f"sm{i}", [128, 8, 4], F32) for i in range(2)]
            m2 = A.mark()
            hT = A.alloc("hTs", [128, 8, NT], BF16)
            S.dma("sp", lambda e: e.dma_start(out=alibi[:, :, :], in_=d_alibi.rearrange("p (h k) -> p h k", h=16)),
                  writes=[alibi.name])
            prenorm(hT, li, 0, 0, tmp)

            def dstq(f, t, ps, b):
                act(qT[:, f, t * 512:(t + 1) * 512], ps, AF.Copy, reads=[pk(b)], writes=[(qT.name, f, t)])
            proj_fm(dstq, d_wq[j].rearrange("(kc p) n -> p kc n", p=128), 2, hT)
            S.barrier()
            A.release(m2)
            sinks = C("sinks", j * 16, (j + 1) * 16)
            it = 0
            LV = DBG["swa"]
            if LV < 99:
                S.op("dve", lambda e: e.memset(oT[:, :, :], 0.0), writes=[oT.name])
            for n in range(NCK):
                firstblk = (half == 0 and n == 0)
                nk = 128 if firstblk else 256
                koff = n * 128 + (128 if firstblk else 0)
                nkb = nk // 128
                for jg in range(4):
                    i2 = it % 2
                    it += 1
                    scb, Pb, Pnb, PTb, smb = sc[i2], Pm[i2], Pn[i2], PT[i2], sm[i2]
                    b = bank2()
                    psc = psum_h[:, b * 512:(b + 2) * 512].rearrange("p (g k) -> p g k", g=4)

                    def fsc(e, jg=jg, n=n, nk=nk, koff=koff, psc=psc):
                        r = None
                        for s_ in range(4):
                            g = (0, 2, 1, 3)[s_]
                            hq = 4 * jg + g
                            ch, po = hq // 2, (hq % 2) * 64
                            r = e.matmul(psc[:, s_, 0:nk], lhsT=qT[po:po + 64, ch, n * 128:(n + 1) * 128],
                                         rhs=kTd[po:po + 64, jg, koff:koff + nk], start=True, stop=True)
                        return r
                    if LV < 1:
                        continue
                    S.op("pe", fsc, reads=[(qT.name, 2 * jg, n // 4), (qT.name, 2 * jg + 1, n // 4), kTd.name],
                         writes=[pk(b), pk(b + 1)])
                    for gh in range(2):
                        stt(scb[:, 2 * gh:2 * gh + 2, 0:nk], psc[:, 2 * gh:2 * gh + 2, 0:nk], 0.125,
                            alibi[:, 4 * jg + 2 * gh:4 * jg + 2 * gh + 2, 256 - nk:256], ALU.mult, ALU.add,
                            reads=[pk(b + gh), alibi.name], writes=[(scb.name, gh)])
                    if LV < 2:
                        continue
                    S.op("dve", lambda e, scb=scb, smb=smb, nk=nk: e.tensor_reduce(out=smb[:, 0, :], in_=scb[:, :, 0:nk], axis=AX.X, op=ALU.max),
                         reads=[scb.name], writes=[(smb.name, 0)])
                    tt(smb[:, 1, :], smb[:, 0, :], sinks[:, 4 * jg:4 * jg + 4], ALU.max, reads=[(smb.name, 0), cst.name],
                       writes=[(smb.name, 1)])
                    ts(smb[:, 2, :], smb[:, 1, :], -1.0, ALU.mult, reads=[(smb.name, 1)], writes=[(smb.name, 2)])
                    if LV < 3:
                        continue
                    for g in range(4):
                        act(Pb[:, g, 0:nk], scb[:, g, 0:nk], AF.Exp, reads=[scb.name, (smb.name, 2)],
                            writes=[(Pb.name, g), (smb.name, 3, g)], bias=smb[:, 2, g:g + 1], accum=smb[:, 3, g:g + 1])
                    tt(smb[:, 4, :], smb[:, 2, :], sinks[:, 4 * jg:4 * jg + 4], ALU.add, reads=[(smb.name, 2), cst.name],
                       writes=[(smb.name, 4)])
                    act(smb[:, 5, :], smb[:, 4, :], AF.Exp, reads=[(smb.name, 4)], writes=[(smb.name, 5)])
                    tt(smb[:, 6, :], smb[:, 5, :], smb[:, 3, :], ALU.add, reads=[(smb.name, 5), (smb.name, 3)],
                       writes=[(smb.name, 6)])
                    S.op("dve", lambda e, smb=smb: e.reciprocal(out=smb[:, 7, :], in_=smb[:, 6, :]),
                         reads=[(smb.name, 6)], writes=[(smb.name, 7)])
                    tt(Pnb[:, :, 0:nk], Pb[:, :, 0:nk], smb[:, 7, :].unsqueeze(2).to_broadcast([128, 4, nk]), ALU.mult,
                       reads=[Pb.name, (smb.name, 7)], writes=[Pnb.name])
                    if LV < 4:
                        continue
                    bt = bank()
                    ppt = PSB(bt)

                    def ftr(e, Pnb=Pnb, ppt=ppt, nkb=nkb):
                        r = None
                        for g in range(4):
                            for kb in range(nkb):
                                i = g * 2 + kb
                                r = e.transpose(ppt[:, i * 128:(i + 1) * 128], Pnb[:, g, kb * 128:(kb + 1) * 128], identb[:, :])
                        return r
                    S.op("pe", ftr, reads=[Pnb.name, identb.name], writes=[pk(bt)])
                    if nkb == 2:
                        act(PTb[:, :, :], ppt.rearrange("p (i q) -> p i q", i=8), AF.Copy, reads=[pk(bt)], writes=[PTb.name])
                    else:
                        S.op("act", lambda e, PTb=PTb, ppt=ppt: e.activation(
                            out=PTb[:, :, :].rearrange("p (g k) q -> p g k q", k=2)[:, :, 0, :],
                            in_=ppt.rearrange("p (g k q) -> p g k q", g=4, k=2)[:, :, 0, :], func=AF.Copy),
                            reads=[pk(bt)], writes=[PTb.name])
                    if LV < 5:
                        continue
                    bo = bank()
                    pot = PS(bo, 256).rearrange("p (c q) -> p c q", c=2)

                    def fpv(e, jg=jg, n=n, nkb=nkb, PTb=PTb, pot=pot, firstblk=firstblk):
                        r = None
                        for s_ in range(4):
                            g = (0, 2, 1, 3)[s_]
                            po = (g % 2) * 64
                            for kb in range(nkb):
                                vt = (n + 1) if firstblk else (n + kb)
                                r = e.matmul(pot[po:po + 64, g // 2, :], lhsT=vtok[:, vt, jg * 64:(jg + 1) * 64],
                                             rhs=PTb[:, s_ * 2 + kb, :], start=(kb == 0), stop=(kb == nkb - 1))
                        return r
                    S.op("pe", fpv, reads=[PTb.name, vtok.name], writes=[pk(bo)])
                    act(oT[:, 2 * jg:2 * jg + 2, n * 128:(n + 1) * 128], pot, AF.Copy, reads=[pk(bo)],
                        writes=[(oT.name, 2 * jg, n // 4, n), (oT.name, 2 * jg + 1, n // 4, n)])
                if n % 2 == 1:
                    bg()
            S.barrier()
            A.release(m1)
            yb = A.alloc("ybs", [128, 8, NT], F32)
            outproj_post(oT, d_wo[j].rearrange("(c p) n -> p c n", p=128), li, tmp, yb)
            S.barrier()
            A.release(m0)

        def run_threads(gens):
            gens = [g for g in gens if g is not None]
            while gens:
                for g in list(gens):
                    try:
                        next(g)
                    except StopIteration:
                        gens.remove(g)

        def gdn_stage(li, half):
            m_save = A.mark()
            A.release(kv_mark)
            m0 = A.mark()
            tmp = alloc_tmp()
            hT = A.alloc("hTg", [128, 8, NT], BF16)
            oT = A.alloc("oTg", [128, 8, NT], BF16)
            G = {k: A.alloc("g_" + k, [128, NCK, 8], F32) for k in
                 ["xa", "ax", "e", "l", "sp", "glog", "beta", "gcum", "ngc", "eg", "nbg", "nbeta", "kgs", "egl", "t1"]}
            negA = A.alloc("negA", [128, 8], F32)
            prenorm(hT, li, 0, 0, tmp)
            S.dma("pool", lambda e: e.dma_start(out=wab[:, :, :], in_=d_wab[li].rearrange("(kc p) n -> p kc n", p=128)),
                  writes=[wab.name])
            b = bank()
            pab = PS(b, NCK * 16).rearrange("p (c n) -> p c n", n=16)
            for ck in range(NCK):
                S.op("pe", mmgroup(pab[:, ck, :], [(hT[:, kc, ck * 128:(ck + 1) * 128], wab[:, kc, :]) for kc in range(8)]),
                     reads=[wab.name] + [(hT.name, kc, ck // 4) for kc in range(8)], writes=[pk(b)])
            gk = lambda k: [G[k].name]
            act(negA[:, :], C("alog", li * 8, li * 8 + 8), AF.Exp, reads=[cst.name], writes=[negA.name])
            ts(negA[:, :], negA[:, :], -1.0, ALU.mult, reads=[negA.name], writes=[negA.name])
            dtb_b = C("dtb", li * 8, li * 8 + 8).unsqueeze(1).to_broadcast([128, NCK, 8])
            tt(G["xa"][:, :, :], pab[:, :, 0:8], dtb_b, ALU.add, reads=[pk(b), cst.name], writes=gk("xa"))
            act(G["beta"][:, :, :], pab[:, :, 8:16], AF.Sigmoid, reads=[pk(b)], writes=gk("beta"))
            act(G["ax"][:, :, :], G["xa"][:, :, :], AF.Abs, reads=gk("xa"), writes=gk("ax"))
            act(G["e"][:, :, :], G["ax"][:, :, :], AF.Exp, reads=gk("ax"), writes=gk("e"), scale=-1.0)
            act(G["l"][:, :, :], G["e"][:, :, :], AF.Ln, reads=gk("e"), writes=gk("l"), bias=1.0)
            stt(G["sp"][:, :, :], G["xa"][:, :, :], 0.0, G["l"][:, :, :], ALU.max, ALU.add, reads=gk("xa") + gk("l"), writes=gk("sp"))
            tt(G["glog"][:, :, :], G["sp"][:, :, :], negA[:, :].unsqueeze(1).to_broadcast([128, NCK, 8]), ALU.mult,
               reads=gk("sp") + [negA.name], writes=gk("glog"))
            b2 = bank()
            S.op("pe", lambda e: e.matmul(PS(b2, NCK * 8), lhsT=C("triU"), rhs=G["glog"][:, :, :].rearrange("p c h -> p (c h)"),
                                          start=True, stop=True), reads=[cst.name] + gk("glog"), writes=[pk(b2)])
            S.op("dve", lambda e: e.tensor_copy(out=G["gcum"][:, :, :].rearrange("p c h -> p (c h)"), in_=PS(b2, NCK * 8)),
                 reads=[pk(b2)], writes=gk("gcum"))
            ts(G["ngc"][:, :, :], G["gcum"][:, :, :], -1.0, ALU.mult, reads=gk("gcum"), writes=gk("ngc"))
            act(G["eg"][:, :, :], G["gcum"][:, :, :], AF.Exp, reads=gk("gcum"), writes=gk("eg"))
            stt(G["nbg"][:, :, :], G["beta"][:, :, :], -1.0, G["eg"][:, :, :], ALU.mult, ALU.mult, reads=gk("beta") + gk("eg"), writes=gk("nbg"))
            ts(G["nbeta"][:, :, :], G["beta"][:, :, :], -1.0, ALU.mult, reads=gk("beta"), writes=gk("nbeta"))
            b3 = bank()
            S.op("pe", lambda e: e.matmul(PS(b3, NCK * 8), lhsT=ident[:, 127:128].to_broadcast([128, 128]),
                                          rhs=G["gcum"][:, :, :].rearrange("p c h -> p (c h)"), start=True, stop=True),
                 reads=[cst.name] + gk("gcum"), writes=[pk(b3)])
            g2 = lambda k: G[k][:, :, :].rearrange("p c h -> p (c h)")
            tt(g2("t1"), PS(b3, NCK * 8), g2("gcum"), ALU.subtract, reads=[pk(b3)] + gk("gcum"), writes=gk("t1"))
            act(g2("kgs"), g2("t1"), AF.Exp, reads=gk("t1"), writes=gk("kgs"))
            act(g2("egl"), PS(b3, NCK * 8), AF.Exp, reads=[pk(b3)], writes=gk("egl"))

            m1 = A.mark()
            sets = []
            for si in range(2):
                st_ = {k: A.alloc(f"{k}{si}", [128, NT], BF16) for k in ["qT", "kT", "zs"]}
                st_["vbt"] = A.alloc(f"vbt{si}", [128, NCK, 128], BF16)
                st_["kgt"] = A.alloc(f"kgt{si}", [128, NCK, 128], BF16)
                for k in ["TTb", "attnT", "qg"]:
                    st_[k] = A.alloc(f"{k}{si}", [128, NCK, 128], BF16)
                sets.append(st_)
            preb = [A.alloc(f"preb{i}", [128, NT + 8], BF16) for i in range(2)]
            dgw = [A.alloc(f"dgw{i}", [128, 128], BF16) for i in range(12)]
            sil = [A.alloc(f"sil{i}", [128, 512], F32) for i in range(2)]
            vTt = [A.alloc(f"vTt{i}", [128, 512], BF16) for i in range(2)]
            f4 = lambda nm: A.alloc(nm, [128, 4, 128], F32)
            NDT = BF16 if DBG.get("nbf16", 0) else F32
            n4 = lambda nm: A.alloc(nm, [128, 4, 128], NDT)
            L = {"X": [n4("X4a"), n4("X4b")], "Y": [n4("Y4a"), n4("Y4b")], "P": [n4("P4a"), n4("P4b")],
                 "LowS": f4("LowS4"), "UpI": f4("UpI4"), "EGb": f4("EGb4")}
            rr = [A.alloc(f"rr{i}", [128, 128], BF16) for i in range(2)]
            vn = [A.alloc(f"vn{i}", [128, 128], BF16) for i in range(2)]
            Sb = A.alloc("Sb", [128, 128], BF16)
            oraw = A.alloc("oraw", [128, NT], F32)
            cnt = {"sil": 0, "pre": 0}

            def P_head(h, st_):
                qT, kT, zs, vbt, kgt = st_["qT"], st_["kT"], st_["zs"], st_["vbt"], st_["kgt"]
                slot = next_wA()
                dma_w(slot[:, :, :], d_win[li, h].rearrange("(kc p) n -> p kc n", p=128), slot.name)
                for ci in range(3):
                    for jj in range(4):
                        idx = ((li * 3 + ci) * 8 + h) * 4 + jj
                        ts(dgw[ci * 4 + jj][:, :], identb[:, :], C("convw", idx, idx + 1), ALU.mult,
                           reads=[identb.name, cst.name], writes=[dgw[ci * 4 + jj].name])
                for ci in range(3):
                    hidx = (li * 3 + ci) * 8 + h
                    pre = preb[cnt["pre"] % 2]
                    cnt["pre"] += 1
                    S.op("dve", lambda e, hidx=hidx, pre=pre: e.tensor_copy(out=pre[:, 0:3], in_=halo[:, hidx, :]),
                         reads=[(halo.name, hidx)], writes=[(pre.name, "h")])
                    for t in range(NTT):
                        bb = bank()
                        S.op("pe", mmgroup(PS(bb), [(slot[:, kc, ci * 128:(ci + 1) * 128], hT[:, kc, t * 512:(t + 1) * 512])
                                                    for kc in range(8)]),
                             reads=[slot.name] + [(hT.name, kc, t) for kc in range(8)], writes=[pk(bb)])
                        act(pre[:, 3 + t * 512: 3 + (t + 1) * 512], PS(bb), AF.Copy, reads=[pk(bb)], writes=[(pre.name, t)])
                    S.op("dve", lambda e, hidx=hidx, pre=pre: e.tensor_copy(out=halo[:, hidx, :], in_=pre[:, NT:NT + 3]),
                         reads=[(pre.name, NTT - 1)], writes=[(halo.name, hidx)])
                    for t in range(NTT):
                        rd = [(pre.name, t)] + ([(pre.name, t - 1)] if t > 0 else [(pre.name, "h")])
                        bc = bank()
                        S.op("pe", mmgroup(PS(bc), [(dgw[ci * 4 + jj][:, :], pre[:, t * 512 + jj: t * 512 + jj + 512]) for jj in range(4)]),
                             reads=rd + [dgw[ci * 4 + jj].name for jj in range(4)], writes=[pk(bc)])
                        cols = slice(t * 512, (t + 1) * 512)
                        if ci < 2:
                            sl = sil[cnt["sil"] % 2]
                            cnt["sil"] += 1
                            act(sl[:, :], PS(bc), AF.Silu, reads=[pk(bc)], writes=[sl.name])
                            sumsq_rstd(sl, lambda kc, sl=sl: sl.name, slice(0, 512), ones1, tmp["sq"], tmp["rt"], tmp["rs"],
                                       nchunks=1, src3=False)
                            dstT = qT if ci == 0 else kT
                            stt(dstT[:, cols], sl[:, :], (128.0 ** -0.5) if ci == 0 else 1.0, tmp["rs"][:, :], ALU.mult, ALU.mult,
                                reads=[sl.name, tmp["rs"].name], writes=[(dstT.name, t)])
                        else:
                            vt_ = vTt[t % 2]
                            act(vt_[:, :], PS(bc), AF.Silu, reads=[pk(bc)], writes=[vt_.name])
                            bt = bank()
                            ppt = PSB(bt)

                            def ftr(e, vt_=vt_, ppt=ppt):
                                r = None
                                for i in range(4):
                                    r = e.transpose(ppt[:, i * 128:(i + 1) * 128], vt_[:, i * 128:(i + 1) * 128], identb[:, :])
                                return r
                            S.op("pe", ftr, reads=[vt_.name, identb.name], writes=[pk(bt)])
                            for i in range(4):
                                ck = t * 4 + i
                                act(vbt[:, ck, :], ppt[:, i * 128:(i + 1) * 128], AF.Identity, reads=[pk(bt)] + gk("beta"),
                                    writes=[(vbt.name, ck)], scale=G["beta"][:, ck, h:h + 1])
                for t in range(NTT):
                    bb = bank()
                    S.op("pe", mmgroup(PS(bb), [(slot[:, kc, 384:512], hT[:, kc, t * 512:(t + 1) * 512]) for kc in range(8)]),
                         reads=[slot.name] + [(hT.name, kc, t) for kc in range(8)], writes=[pk(bb)])
                    act(zs[:, t * 512:(t + 1) * 512], PS(bb), AF.Silu, reads=[pk(bb)], writes=[(zs.name, t)])
                for t in range(NTT):
                    bt = bank()
                    ppt = PSB(bt)

                    def ftk(e, t=t, ppt=ppt):
                        r = None
                        for i in range(4):
                            c0 = t * 512 + i * 128
                            r = e.transpose(ppt[:, i * 128:(i + 1) * 128], kT[:, c0:c0 + 128], identb[:, :])
                        return r
                    S.op("pe", ftk, reads=[(kT.name, t), identb.name], writes=[pk(bt)])
                    for i in range(4):
                        ck = t * 4 + i
                        act(kgt[:, ck, :], ppt[:, i * 128:(i + 1) * 128], AF.Identity, reads=[pk(bt)] + gk("kgs"),
                            writes=[(kgt.name, ck)], scale=G["kgs"][:, ck, h:h + 1])

            def T_group(h, st_, c0):
                qT, kT = st_["qT"], st_["kT"]
                tq = c0 // 4
                c4 = slice(c0 * 128, (c0 + 4) * 128)
                v3 = lambda ap: ap.rearrange("p (c s) -> p c s", c=4)
                bc3 = lambda ap: ap.unsqueeze(1).to_broadcast([128, 4, 128])
                gsl = lambda k: G[k][:, c0:c0 + 4, h:h + 1].to_broadcast([128, 4, 128])
                LowS, UpI, EGb = L["LowS"], L["UpI"], L["EGb"]

                def mm4(b_, lf, rf):
                    def f(e):
                        r = None
                        for i in range(4):
                            r = e.matmul(PS(b_, 128, i * 128), lhsT=lf(i), rhs=rf(i), start=True, stop=True)
                        return r
                    return f
                bA = bank()
                S.op("pe", mm4(bA, lambda i: G["gcum"][:, c0 + i, h:h + 1].to_broadcast([128, 128]), lambda i: ident),
                     reads=gk("gcum") + [cst.name], writes=[pk(bA)])
                stt(LowS[:, :, :], v3(PS(bA)), -1.0, bc3(C("mnS")), ALU.mult, ALU.add, reads=[pk(bA), cst.name], writes=[LowS.name])
                tt(UpI[:, :, :], v3(PS(bA)), bc3(C("mnUI")), ALU.add, reads=[pk(bA), cst.name], writes=[UpI.name])
                act(EGb[:, :, :], v3(PS(bA)), AF.Exp, reads=[pk(bA)], writes=[EGb.name])
                bK = bank()
                S.op("pe", mm4(bK, lambda i: kT[:, (c0 + i) * 128:(c0 + i + 1) * 128], lambda i: kT[:, (c0 + i) * 128:(c0 + i + 1) * 128]),
                     reads=[(kT.name, tq)], writes=[pk(bK)])
                bQ = bank()
                S.op("pe", mm4(bQ, lambda i: kT[:, (c0 + i) * 128:(c0 + i + 1) * 128], lambda i: qT[:, (c0 + i) * 128:(c0 + i + 1) * 128]),
                     reads=[(kT.name, tq), (qT.name, tq)], writes=[pk(bQ)])
                for i in range(4):
                    act(LowS[:, i, :], LowS[:, i, :], AF.Exp, reads=[LowS.name] + gk("gcum"), writes=[(LowS.name, i)],
                        bias=G["gcum"][:, c0 + i, h:h + 1])
                    act(UpI[:, i, :], UpI[:, i, :], AF.Exp, reads=[UpI.name] + gk("ngc"), writes=[(UpI.name, i)],
                        bias=G["ngc"][:, c0 + i, h:h + 1])
                tt(st_["qg"][:, c0:c0 + 4, :], v3(qT[:, c4]), EGb[:, :, :], ALU.mult, reads=[(qT.name, tq), EGb.name],
                   writes=[(st_["qg"].name, tq)])
                X, Y, P = L["X"][0], L["Y"][0], L["P"][0]
                for i in range(4):
                    stt(X[:, i, :], PS(bK, 128, i * 128), G["nbeta"][:, c0 + i, h:h + 1], LowS[:, i, :], ALU.mult, ALU.mult,
                        reads=[pk(bK), (LowS.name, i)] + gk("nbeta"), writes=[(X.name, i)])
                tt(st_["attnT"][:, c0:c0 + 4, :], v3(PS(bQ)), UpI[:, :, :], ALU.mult, reads=[pk(bQ), UpI.name],
                   writes=[(st_["attnT"].name, tq)])
                bX = bank()

                bfn = (NDT == BF16)
                pX = PSB(bX)[:, 0:512] if bfn else PS(bX)

                def ftx(e, X=X, bX=bX):
                    r = None
                    for i in range(4):
                        r = e.transpose(pX[:, i * 128:(i + 1) * 128], X[:, i, :], identb[:, :] if bfn else ident)
                    return r
                S.op("pe", ftx, reads=[X.name, cst.name, identb.name], writes=[pk(bX)])
                act(Y[:, :, :], v3(pX), AF.Copy, reads=[pk(bX)], writes=[Y.name])
                tt(P[:, :, :], Y[:, :, :], bc3(ident), ALU.add, reads=[Y.name, cst.name], writes=[P.name])
                yield
                cur = 0
                for lev in range(1, 7):
                    nxt = 1 - cur
                    X, Y, P = L["X"][cur], L["Y"][cur], L["P"][cur]
                    Xn, Yn, Pn_ = L["X"][nxt], L["Y"][nxt], L["P"][nxt]
                    bxn = bank()
                    S.op("pe", mm4(bxn, lambda i, Y=Y: Y[:, i, :], lambda i, X=X: X[:, i, :]), reads=[X.name, Y.name], writes=[pk(bxn)])
                    if lev < 6:
                        byn = bank()
                        S.op("pe", mm4(byn, lambda i, X=X: X[:, i, :], lambda i, Y=Y: Y[:, i, :]), reads=[X.name, Y.name], writes=[pk(byn)])
                    act(Xn[:, :, :], v3(PS(bxn)), AF.Copy, reads=[pk(bxn)], writes=[Xn.name])
                    if lev < 6:
                        act(Yn[:, :, :], v3(PS(byn)), AF.Copy, reads=[pk(byn)], writes=[Yn.name])
                    yield
                    bpn = bank()
                    S.op("pe", mm4(bpn, lambda i, Xn=Xn: Xn[:, i, :], lambda i, P=P: P[:, i, :]), reads=[Xn.name, P.name], writes=[pk(bpn)])
                    if lev < 6:
                        tt(Pn_[:, :, :], v3(PS(bpn)), P[:, :, :], ALU.add, reads=[pk(bpn), P.name], writes=[Pn_.name])
                    else:
                        tt(st_["TTb"][:, c0:c0 + 4, :], v3(PS(bpn)), P[:, :, :], ALU.add, reads=[pk(bpn), P.name],
                           writes=[(st_["TTb"].name, tq)])
                    yield
                    cur = nxt

            def T_head(h, st_):
                cut = DBG.get("cut", None)
                S.paranoid = int(DBG.get("paranoid", 0))
                for c0 in range(0, NCK, 4):
                    if cut is None:
                        yield from T_group(h, st_, c0)
                    else:
                        real = S.op
                        cnt_ = [0]

                        def wrapped(eng, fn, reads=(), writes=()):
                            cnt_[0] += 1
                            if cnt_[0] <= cut:
                                return real(eng, fn, reads=reads, writes=writes)
                            return None
                        S.op = wrapped
                        try:
                            for _ in T_group(h, st_, c0):
                                pass
                        finally:
                            S.op = real
                        yield

            def C_head(h, st_):
                S.paranoid = 0
                kT, zs, vbt, kgt = st_["kT"], st_["zs"], st_["vbt"], st_["kgt"]
                sidx = li * 8 + h
                Sf = Scar[:, sidx, :]
                Sk = (Scar.name, sidx)
                S.op("dve", lambda e: e.tensor_copy(out=Sb[:, :], in_=Sf), reads=[Sk], writes=[Sb.name])
                yield
                for ck in range(NCK):
                    r2 = ck % 2
                    cc = slice(ck * 128, (ck + 1) * 128)
                    tq = ck // 4
                    TTb, attnT, qg = st_["TTb"][:, ck, :], st_["attnT"][:, ck, :], st_["qg"][:, ck, :]
                    kTT, kAT, kQG = (st_["TTb"].name, tq), (st_["attnT"].name, tq), (st_["qg"].name, tq)
                    b1 = bank()
                    S.op("pe", lambda e, b1=b1, cc=cc: e.matmul(PS(b1, 128), lhsT=kT[:, cc], rhs=Sb[:, :], start=True, stop=True),
                         reads=[(kT.name, tq), Sb.name], writes=[pk(b1)])
                    yield
                    stt(rr[r2][:, :], PS(b1, 128), G["nbg"][:, ck, h:h + 1], vbt[:, ck, :], ALU.mult, ALU.add,
                        reads=[pk(b1), (vbt.name, ck)] + gk("nbg"), writes=[rr[r2].name])
                    yield
                    b2_ = bank()
                    S.op("pe", lambda e, b2_=b2_, r2=r2, TTb=TTb: e.matmul(PS(b2_, 128), lhsT=TTb, rhs=rr[r2][:, :], start=True, stop=True),
                         reads=[kTT, rr[r2].name], writes=[pk(b2_)])
                    yield
                    act(vn[r2][:, :], PS(b2_, 128), AF.Copy, reads=[pk(b2_)], writes=[vn[r2].name])
                    yield
                    b3_ = bank()
                    S.op("pe", mmgroup(PS(b3_, 128), [(Sb[:, :], qg), (vn[r2][:, :], attnT)]),
                         reads=[Sb.name, kQG, vn[r2].name, kAT], writes=[pk(b3_)])
                    b4_ = bank()
                    S.op("pe", lambda e, b4_=b4_, ck=ck, r2=r2: e.matmul(PS(b4_, 128), lhsT=kgt[:, ck, :], rhs=vn[r2][:, :], start=True, stop=True),
                         reads=[(kgt.name, ck), vn[r2].name], writes=[pk(b4_)])
                    yield
                    stt(Sf, Sf, G["egl"][:, ck, h:h + 1], PS(b4_, 128), ALU.mult, ALU.add, reads=[Sk, pk(b4_)] + gk("egl"), writes=[Sk])
                    act(oraw[:, cc], PS(b3_, 128), AF.Copy, reads=[pk(b3_)], writes=[(oraw.name, tq, ck)])
                    yield
                    S.op("act", lambda e: e.activation(out=Sb[:, :], in_=Sf, func=AF.Copy), reads=[Sk], writes=[Sb.name])
                    yield
                for t in range(NTT):
                    cols = slice(t * 512, (t + 1) * 512)
                    sumsq_rstd(oraw, lambda kc, t=t: (oraw.name, t), cols, onesV, tmp["sq"], tmp["rt"], tmp["rs"], nchunks=1, src3=False)
                    yield
                    tm = tmp["f"][t % 2]
                    stt(tm[:, :], oraw[:, cols], C("onorm", li, li + 1), tmp["rs"][:, :], ALU.mult, ALU.mult,
                        reads=[(oraw.name, t), tmp["rs"].name, cst.name], writes=[tm.name])
                    tt(oT[:, h, cols], tm[:, :], zs[:, cols], ALU.mult, reads=[tm.name, (zs.name, t)], writes=[(oT.name, h, t)])
                    yield

            GL = DBG.get("gdn", 3)
            if GL < 3:
                S.op("dve", lambda e: e.memset(oT[:, :, :], 0.0), writes=[oT.name])
            OV = DBG.get("ov", 0)
            for k in range(9):
                if k < 8:
                    P_head(k, sets[k % 2])
                if OV:
                    th = []
                    if k < 8 and GL >= 2:
                        th.append(T_head(k, sets[k % 2]))
                    if k >= 1 and GL >= 3:
                        th.append(C_head(k - 1, sets[(k - 1) % 2]))
                    run_threads(th)
                else:
                    if k < 8 and GL >= 2:
                        run_threads([T_head(k, sets[k % 2])])
                    if k < 8 and GL >= 3:
                        run_threads([C_head(k, sets[k % 2])])
                bg()
            S.barrier()
            A.release(m1)
            yb = A.alloc("ybg", [128, 8, NT], F32)
            outproj_post(oT, d_wout[li].rearrange("(c p) n -> p c n", p=128), li, tmp, yb)
            S.barrier()
            A.release(m0)
            A.top = m_save

        for half in halves:
            for kc in range(8):
                S.dma("sp", lambda e, kc=kc, half=half: e.dma_start(out=xT[:, kc, :], in_=d_xT[kc][:, half * NT:(half + 1) * NT]),
                      writes=[(xT.name, kc)])
            for st in stages:
                if st == "kv":
                    bg_flush(4)
                    kv_stage(half)
                else:
                    li = int(st[-1])
                    bg_flush(li)
                    if st.startswith("mlp"):
                        mlp_stage(li, half)
                    elif li < 2:
                        gdn_stage(li, half)
                    else:
                        swa_stage(li, half)
            for kc in range(8):
                S.dma("sp", lambda e, kc=kc, half=half: e.dma_start(out=d_out[kc][:, half * NT:(half + 1) * NT], in_=xT[:, kc, :]),
                      reads=[(xT.name, kc)])
            S.barrier()
        S.finish()
        S.emit()
    return nc


def _alibi_table():
    q = np.arange(128)[:, None]
    k = np.arange(256)[None, :]
    dist = (q + 128 - k).astype(np.float32)
    valid = (dist >= 0) & (dist < 128)
    slopes = np.exp2(-8.0 * np.arange(1, 17, dtype=np.float32) / 16.0)
    tab = np.where(valid[:, None, :], -slopes[None, :, None] * dist[:, None, :], NEG).astype(np.float32)
    perm = np.array([4 * (h // 4) + (0, 2, 1, 3)[h % 4] for h in range(16)])
    tab = tab[:, perm, :]
    return np.ascontiguousarray(tab.reshape(128, 16 * 256))


def prep_inputs(inp):
    f = lambda a: np.ascontiguousarray(np.asarray(a, dtype=np.float32))
    x = f(inp["x"]); c = f(inp["c"])
    B = x.shape[0]
    w_in = f(inp["gdn_w_in"])
    w_in_h = np.stack([np.stack([np.concatenate([w_in[l][:, cidx * 1024 + h * 128: cidx * 1024 + (h + 1) * 128] for cidx in range(4)], axis=1)
                                 for h in range(8)]) for l in range(2)])
    w_ab = np.ascontiguousarray(w_in[:, :, 4096:4112])
    w_kv = f(inp["w_kv"])
    w_kd = np.concatenate([np.concatenate([w_kv[:, j * 64:(j + 1) * 64]] * 2, axis=1) for j in range(4)], axis=1)
    w_v = np.ascontiguousarray(w_kv[:, 256:512])
    w2 = f(inp["mlp_w2"])
    w2r = np.ascontiguousarray(w2.reshape(4, 32, 128, 8, 128).transpose(0, 3, 2, 1, 4))
    shared = {
        "alibi": _alibi_table(),
        "mod_w": f(inp["mod_w"]), "kv_mod_w": f(inp["kv_mod_w"]),
        "w_in_h": np.ascontiguousarray(w_in_h), "w_ab": w_ab, "w_out": f(inp["gdn_w_out"]),
        "w_kd": np.ascontiguousarray(w_kd), "w_v": w_v, "w_q": f(inp["swa_w_q"]), "w_o": f(inp["swa_w_o"]),
        "w1": f(inp["mlp_w1"]), "w2r": w2r,
    }
    cst = np.zeros((128, NCST), np.float32)

    def put(name, arr):
        o, n = CST_OFF[name]
        assert arr.shape == (128, n), (name, arr.shape)
        cst[:, o:o + n] = arr
    put("modb", f(inp["mod_b"]).reshape(4, 48, 128).transpose(2, 0, 1).reshape(128, 192))
    put("kvmodb", f(inp["kv_mod_b"]).reshape(16, 128).T)
    put("ng", f(inp["norm_g"]).reshape(4, 4, 8, 128).transpose(3, 0, 1, 2).reshape(128, 128))
    put("kvng", f(inp["kv_norm"]).reshape(8, 128).T)
    cw = f(inp["gdn_conv"]).reshape(2, 4, 3, 8, 128)
    put("convw", cw.transpose(4, 0, 2, 3, 1).reshape(128, 192))
    put("onorm", f(inp["gdn_onorm"]).T)
    put("alog", np.broadcast_to(f(inp["gdn_a_log"]).reshape(1, 16), (128, 16)))
    put("dtb", np.broadcast_to(f(inp["gdn_dt_bias"]).reshape(1, 16), (128, 16)))
    perm = np.array([4 * (h // 4) + (0, 2, 1, 3)[h % 4] for h in range(16)])
    put("sinks", np.broadcast_to(f(inp["swa_sinks"])[:, perm].reshape(1, 32), (128, 32)))
    put("ident", np.eye(128, dtype=np.float32))
    s = np.arange(128)[:, None]; cidx = np.arange(128)[None, :]
    put("triU", (s <= cidx).astype(np.float32))
    put("mnS", np.where(s > cidx, 0.0, NEG).astype(np.float32))
    put("mnUI", np.where(cidx >= s, 0.0, NEG).astype(np.float32))
    per_core = []
    for b in range(B):
        cc = cst.copy()
        o, n = CST_OFF["cT"]
        cc[:, o:o + n] = c[b].reshape(8, 128).T
        per_core.append({"xT": np.ascontiguousarray(x[b].T.reshape(8, 128, SEQ)), "cst": cc})
    return shared, per_core


_NC_CACHE = {}


def run_stages(inp, stages=None, halves=(0, 1), cores=None):
    shared, per_core = prep_inputs(inp)
    key = (tuple(stages) if stages else None, tuple(halves))
    if key not in _NC_CACHE:
        _NC_CACHE[key] = build_program(stages, halves)
    nc = _NC_CACHE[key]
    cores = list(range(len(per_core))) if cores is None else cores
    in_maps = [dict(shared, **per_core[b]) for b in cores]
    res = run_bass_kernel_spmd(nc, in_maps, core_ids=list(range(len(cores))))
    outs = [np.asarray(r["outT"]).reshape(D, SEQ).T for r in res.results]
    return np.stack(outs).astype(np.float32)


def kernel(**inputs):
    return run_stages(inputs)
```
